# Optimizing a Trainium2 kernel written in Bass

```python
import math
import jax, jax.numpy as jnp
from jax import lax
import numpy as np

D_MODEL = 4096
BATCH = 2
SEQ = 8192
DEPTH = 2

PLE_DIM = 256
D_FF = 11008
ATTN_QK_DIM = 64
ATTN_V_DIM = 2 * ATTN_QK_DIM
ATTN_WIDTH = D_MODEL // 2
N_ATTN_HEADS = ATTN_WIDTH // ATTN_V_DIM
QK_WIDTH = N_ATTN_HEADS * 2 * ATTN_QK_DIM
CONV_WIDTH = D_MODEL - ATTN_WIDTH
CONV_GROUPS = 16
CONV_GROUP_DIM = CONV_WIDTH // CONV_GROUPS
CONV_K = 3
MIX_WIDTH = ATTN_WIDTH + CONV_WIDTH
SPLITS = (
    QK_WIDTH,
    2 * QK_WIDTH,
    2 * QK_WIDTH + ATTN_WIDTH,
    2 * QK_WIDTH + ATTN_WIDTH + CONV_WIDTH,
    2 * QK_WIDTH + ATTN_WIDTH + 2 * CONV_WIDTH,
)
IN_PROJ_WIDTH = 2 * QK_WIDTH + ATTN_WIDTH + 3 * CONV_WIDTH
Q_BLOCK = 128
EPS = 1e-6
HALF_STEP = 0.5

kernel_name = "hymba_diffattn_shortconv_macaron_ple"


def _rmsnorm(x, gain):
    x32 = x.astype(jnp.float32)
    y = x32 * lax.rsqrt(jnp.mean(x32 * x32, axis=-1, keepdims=True) + EPS)
    return y.astype(x.dtype) * gain


def _swiglu(x, w_gate, w_up, w_down):
    return (jax.nn.silu(x @ w_gate) * (x @ w_up)) @ w_down


def _alibi_slopes(n_heads):
    return jnp.exp2(-8.0 * jnp.arange(1, n_heads + 1, dtype=jnp.float32) / n_heads)


def _lambda_init(layer_idx):
    return 0.8 - 0.6 * math.exp(-0.3 * layer_idx)


def _diff_attention(q, k, v, lam, slopes):
    bsz, seq = q.shape[0], q.shape[1]
    n_blocks = seq // Q_BLOCK
    key_pos = jnp.arange(seq, dtype=jnp.int32)
    scale = ATTN_QK_DIM ** -0.5

    def one_block(blk):
        start = blk * Q_BLOCK
        qb = lax.dynamic_slice_in_dim(q, start, Q_BLOCK, axis=1)
        s = jnp.einsum('bqhmd,bkhmd->bhmqk', qb, k,
                       preferred_element_type=jnp.float32) * scale
        dist = (start + jnp.arange(Q_BLOCK, dtype=jnp.int32))[:, None] - key_pos[None, :]
        bias = -slopes[:, None, None] * dist.astype(jnp.float32)
        s = jnp.where(dist[None, None, None] >= 0, s + bias[None, :, None], -jnp.inf)
        prob = jax.nn.softmax(s, axis=-1)
        a = prob[:, :, 0] - lam * prob[:, :, 1]
        return jnp.einsum('bhqk,bkhd->bqhd', a.astype(v.dtype), v)

    out = lax.map(one_block, jnp.arange(n_blocks, dtype=jnp.int32))
    return jnp.moveaxis(out, 0, 1).reshape(bsz, seq, N_ATTN_HEADS, ATTN_V_DIM)


def _short_conv(u, w):
    seq = u.shape[1]
    up = jnp.pad(u, ((0, 0), (CONV_K - 1, 0), (0, 0)))
    acc = up[:, 0:seq] * w[0]
    for j in range(1, CONV_K):
        acc = acc + up[:, j:j + seq] * w[j]
    return acc


def setup_inputs(seed: int = 0) -> dict:
    key = jax.random.key(seed)
    ks = jax.random.split(key, 26)

    def dense(k, shape, fan_in):
        return jax.random.normal(k, shape, jnp.float32) * (fan_in ** -0.5)

    def gain(k, shape):
        return 1.0 + 0.1 * jax.random.normal(k, shape, jnp.float32)

    def small(k, shape, scale=0.1):
        return scale * jax.random.normal(k, shape, jnp.float32)

    L = DEPTH
    return {
        "x": jax.random.normal(ks[0], (BATCH, SEQ, D_MODEL), jnp.float32),
        "p": jax.random.normal(ks[1], (DEPTH, BATCH, SEQ, PLE_DIM), jnp.float32),
        "ffn1_norm": gain(ks[2], (L, D_MODEL)),
        "ffn1_w_gate": dense(ks[3], (L, D_MODEL, D_FF), D_MODEL),
        "ffn1_w_up": dense(ks[4], (L, D_MODEL, D_FF), D_MODEL),
        "ffn1_w_down": dense(ks[5], (L, D_FF, D_MODEL), D_FF),
        "mix_norm": gain(ks[6], (L, D_MODEL)),
        "w_in": dense(ks[7], (L, D_MODEL, IN_PROJ_WIDTH), D_MODEL),
        "q_norm": gain(ks[8], (L, ATTN_QK_DIM)),
        "k_norm": gain(ks[9], (L, ATTN_QK_DIM)),
        "lambda_q1": small(ks[10], (L, ATTN_QK_DIM)),
        "lambda_k1": small(ks[11], (L, ATTN_QK_DIM)),
        "lambda_q2": small(ks[12], (L, ATTN_QK_DIM)),
        "lambda_k2": small(ks[13], (L, ATTN_QK_DIM)),
        "attn_subln": gain(ks[14], (L, ATTN_V_DIM)),
        "conv_w": dense(ks[15], (L, CONV_K, CONV_WIDTH), CONV_K),
        "conv_norm": gain(ks[16], (L, CONV_GROUP_DIM)),
        "w_out": dense(ks[17], (L, MIX_WIDTH, D_MODEL), MIX_WIDTH),
        "ffn2_norm": gain(ks[18], (L, D_MODEL)),
        "ffn2_w_gate": dense(ks[19], (L, D_MODEL, D_FF), D_MODEL),
        "ffn2_w_up": dense(ks[20], (L, D_MODEL, D_FF), D_MODEL),
        "ffn2_w_down": dense(ks[21], (L, D_FF, D_MODEL), D_FF),
        "ple_w_proj": dense(ks[22], (L, PLE_DIM, D_MODEL), PLE_DIM),
        "ple_post_norm": gain(ks[23], (L, D_MODEL)),
        "ple_gate_norm": gain(ks[24], (L, D_MODEL)),
        "ple_w_gate": dense(ks[25], (L, D_MODEL, D_MODEL), D_MODEL),
    }


def reference(x, p, ffn1_norm, ffn1_w_gate, ffn1_w_up, ffn1_w_down,
              mix_norm, w_in, q_norm, k_norm, lambda_q1, lambda_k1, lambda_q2, lambda_k2,
              attn_subln, conv_w, conv_norm, w_out,
              ffn2_norm, ffn2_w_gate, ffn2_w_up, ffn2_w_down,
              ple_w_proj, ple_post_norm, ple_gate_norm, ple_w_gate):
    bsz, seq = x.shape[0], x.shape[1]
    slopes = _alibi_slopes(N_ATTN_HEADS)
    for i in range(DEPTH):
        x = x + HALF_STEP * _swiglu(_rmsnorm(x, ffn1_norm[i]),
                                    ffn1_w_gate[i], ffn1_w_up[i], ffn1_w_down[i])

        h = _rmsnorm(x, mix_norm[i])
        z = h @ w_in[i]
        q, k, v, g_b, g_c, u = jnp.split(z, SPLITS, axis=-1)

        q = _rmsnorm(q.reshape(bsz, seq, N_ATTN_HEADS, 2, ATTN_QK_DIM), q_norm[i])
        k = _rmsnorm(k.reshape(bsz, seq, N_ATTN_HEADS, 2, ATTN_QK_DIM), k_norm[i])
        v = v.reshape(bsz, seq, N_ATTN_HEADS, ATTN_V_DIM)
        lam_init = _lambda_init(i)
        lam = (jnp.exp(jnp.sum(lambda_q1[i].astype(jnp.float32) * lambda_k1[i].astype(jnp.float32)))
               - jnp.exp(jnp.sum(lambda_q2[i].astype(jnp.float32) * lambda_k2[i].astype(jnp.float32)))
               + lam_init)
        attn = _diff_attention(q, k, v, lam, slopes)
        attn = (_rmsnorm(attn, attn_subln[i]) * (1.0 - lam_init)).reshape(bsz, seq, ATTN_WIDTH)

        y = g_b * _short_conv(g_c * u, conv_w[i])
        y = _rmsnorm(y.reshape(bsz, seq, CONV_GROUPS, CONV_GROUP_DIM),
                     conv_norm[i]).reshape(bsz, seq, CONV_WIDTH)

        x = x + jnp.concatenate([attn, y], axis=-1) @ w_out[i]

        x = x + HALF_STEP * _swiglu(_rmsnorm(x, ffn2_norm[i]),
                                    ffn2_w_gate[i], ffn2_w_up[i], ffn2_w_down[i])

        e = _rmsnorm(p[i] @ ple_w_proj[i], ple_post_norm[i])
        gate = jax.nn.sigmoid(_rmsnorm(x, ple_gate_norm[i]) @ ple_w_gate[i])
        x = x + gate * e
    return x
```

```python
import math
from collections import deque
from contextlib import ExitStack

import numpy as np
import concourse.bass as bass
import concourse.mybir as mybir
from concourse.bass_utils import run_bass_kernel_spmd

F32 = mybir.dt.float32
BF16 = mybir.dt.bfloat16
ALU = mybir.AluOpType
AF = mybir.ActivationFunctionType
EPS = 1e-6
NEG = -30000.0


class Cfg:
    def __init__(self, D=4096, DFF=11008, S=8192, B=2, L=2, PLE=256, T=512, FPART=32, NB=5,
                 PIECE=32 * 1024 * 1024):
        self.D, self.DFF, self.S, self.B, self.L, self.PLE, self.T = D, DFF, S, B, L, PLE, T
        self.NC = 8
        self.RPS = self.NC
        self.NSQ = B
        self.TPS = S // self.NC
        self.NTS = self.TPS // T
        self.TPC = self.NSQ * self.TPS
        self.NT = self.NSQ * self.NTS
        self.KC = D // 128
        self.FC = DFF // 128
        self.H = (D // 2) // 128
        self.G = self.H
        self.TS = T // 128
        self.CPR = self.TPS // 128
        self.VW = min(512, self.H * 128)
        self.VG = self.H * 128 // self.VW
        self.HPG = self.VW // 128
        self.VKB = min(8, self.KC)
        self.VKQ = self.KC // self.VKB
        self.PO = min(16, self.KC)
        self.PS = self.KC // self.PO
        self.PKC = PLE // 128
        self.FPART = FPART
        self.parts = []
        f = 0
        while f < self.FC:
            n = min(FPART, self.FC - f)
            self.parts.append((f, n))
            f += n
        self.NB = NB
        self.SLABW = max(self.KC * 128, FPART * 128, self.VKB * self.VW, self.PO * self.PKC * 128,
                         2 * self.TPS)
        self.PIECE = PIECE
        self.SCRB = max(FPART * T * 2, 48 * T + 4 * self.VW + 1024)
        self.NE = (self.NTS - 1) * self.TS + self.CPR
        self.NDD = max(1, (self.NTS - 1) * self.TS)
        self.slopes = [2.0 ** (-8.0 * h / self.H) for h in range(1, self.H + 1)]
        self.NPL = 5 * self.KC + 6 + 3 * self.G
        o = 0
        self.cs_avg = o; o += 128
        self.cs_blk = o; o += 128
        self.cs_lo = o; o += 128
        self.cs_hi = o; o += 128
        self.cs_iota = o; o += T
        self.cs_db = o; o += T + (self.TS - 1) * 128
        self.cs_own = o; o += self.H * self.NDD
        self.cs_eps = o; o += 1
        self.NCS = o
        self.cc_al = 0
        self.cc_sel = self.H * (self.RPS - 1) * self.NE
        self.NCC = self.cc_sel + self.RPS

    def lam_init(self, l):
        return 0.8 - 0.6 * math.exp(-0.3 * l)


def weight_plan(cfg):
    c = cfg
    plans = []
    for l in range(c.L):
        sl = []
        for ff in (1, 2):
            if ff == 2:
                for oc in range(c.KC):
                    sl.append((("o", oc), c.KC * 128))
            for (f0, n) in c.parts:
                for fc in range(f0, f0 + n):
                    sl.append((("g", ff, fc), c.KC * 128))
                    sl.append((("u", ff, fc), c.KC * 128))
                for oc in range(c.KC):
                    sl.append((("d", ff, f0, n, oc), n * 128))
            if ff == 1:
                for h in range(c.H):
                    sl.append((("q", h), c.KC * 128))
                for h in range(c.H):
                    sl.append((("k", h), c.KC * 128))
                for vg in range(c.VG):
                    for kq in range(c.VKQ):
                        sl.append((("v", vg, kq), c.VKB * c.VW))
                for g in range(c.G):
                    for nm in ("B", "C", "U"):
                        sl.append((("c" + nm, g), c.KC * 128))
        for s in range(c.PS):
            sl.append((("pp", s), c.PO * c.PKC * 128))
        for oc in range(c.KC):
            sl.append((("pg", oc), c.KC * 128))
        plans.append(sl)
    return plans


def piece_plan(cfg):
    pieces, index = [], {}
    for l, sl in enumerate(weight_plan(cfg)):
        cur = None
        for key, w in sl:
            n = 128 * w
            if cur is None or cur["size"] + n > cfg.PIECE:
                cur = dict(layer=l, slabs=[], size=0)
                pieces.append(cur)
            cur["slabs"].append((key, w, cur["size"]))
            index[(l, key)] = (len(pieces) - 1, cur["size"], w)
            cur["size"] += n
    for p in pieces:
        assert p["size"] % (8 * 2048) == 0
    return pieces, index


def host_slab(cfg, W, l, key):
    c = cfg
    k = key[0]

    def fa(M, kc0, n, oc):
        blk = M[kc0 * 128:(kc0 + n) * 128, oc * 128:(oc + 1) * 128]
        return blk.reshape(n, 128, 128).transpose(1, 0, 2).reshape(128, n * 128)

    if k in ("g", "u"):
        M = W["ffn%d_w_%s" % (key[1], "gate" if k == "g" else "up")][l]
        return fa(M, 0, c.KC, key[2])
    if k == "d":
        _, ff, f0, n, oc = key
        return fa(W["ffn%d_w_down" % ff][l], f0, n, oc)
    if k == "o":
        return fa(W["w_out"][l], 0, c.KC, key[1])
    if k == "pg":
        return fa(W["ple_w_gate"][l], 0, c.KC, key[1])
    Win = W["w_in"][l]
    QW = c.H * 128
    if k == "q":
        return fa(Win, 0, c.KC, key[1])
    if k == "k":
        return fa(Win[:, QW:], 0, c.KC, key[1])
    if k == "v":
        _, vg, kq = key
        blk = Win[kq * c.VKB * 128:(kq + 1) * c.VKB * 128, 2 * QW + vg * c.VW: 2 * QW + (vg + 1) * c.VW]
        return blk.reshape(c.VKB, 128, c.VW).transpose(1, 0, 2).reshape(128, c.VKB * c.VW)
    if k in ("cB", "cC", "cU"):
        base = 3 * QW + {"cB": 0, "cC": 1, "cU": 2}[k] * c.G * 128
        return fa(Win[:, base:], 0, c.KC, key[1])
    if k == "pp":
        s = key[1]
        Wp = W["ple_w_proj"][l]
        blk = Wp.reshape(c.PKC, 128, c.KC, 128)[:, :, s * c.PO:(s + 1) * c.PO, :]
        return blk.transpose(1, 2, 0, 3).reshape(128, c.PO * c.PKC * 128)
    raise KeyError(key)


class Ticket:
    __slots__ = ("sem", "val", "eng")

    def __init__(self, sem, val, eng):
        self.sem, self.val, self.eng = sem, val, eng


class Res:
    __slots__ = ("w", "r", "scr", "epoch")

    def __init__(self, scr=False):
        self.w = None
        self.r = {}
        self.scr = scr
        self.epoch = 0


class DSem:
    def __init__(self, handle):
        self.h = handle
        self.cnt = 0


class Eng:
    SEM_LIMIT = 30000

    def __init__(self, name, sems):
        self.name = name
        self.sems = list(sems)
        self.sem = self.sems.pop(0)
        self.cnt = 0
        self.waited = {}
        self.prog = []
        self.last = None

    def wait(self, tickets):
        best = {}
        for t in tickets:
            if t is None:
                continue
            k = id(t.sem)
            if k not in best or best[k].val < t.val:
                best[k] = t
        for k, t in best.items():
            if self.waited.get(k, 0) < t.val:
                self.waited[k] = t.val
                self.prog.append(lambda e, s=t.sem, v=t.val: e.wait_ge(s, v))

    def emit(self, fn, signal=True):
        if signal:
            if self.cnt >= self.SEM_LIMIT and self.sems:
                self.sem = self.sems.pop(0)
                self.cnt = 0
            self.cnt += 1
            self.prog.append(lambda e, fn=fn, s=self.sem: fn(e).then_inc(s, 1))
            self.last = Ticket(self.sem, self.cnt, self)
            return self.last
        self.prog.append(fn)
        return None


class Gen:
    def __init__(self, cfg):
        self.c = cfg
        self.nc = bass.Bass("TRN2", target_bir_lowering=False)
        self.es = ExitStack()
        self.epoch = 0
        self.barrier = []
        self.scr_dma = {}
        self.pending_stores = {}
        self.nsem = 0
        self.all_sems = []

    def sem(self, name):
        self.nsem += 1
        h = self.es.enter_context(self.nc.semaphore(name))
        self.all_sems.append(h)
        return h

    def dsem(self, name):
        return DSem(self.sem(name))

    def sb(self, name, shape, dt):
        return self.es.enter_context(self.nc.sbuf_tensor(name, shape, dt))

    def deps(self, reads, writes):
        d = []
        for r in reads:
            if r.scr and r.epoch < self.epoch:
                d.extend(self.barrier)
            d.append(r.w)
        for w in writes:
            if w.scr and w.epoch < self.epoch:
                d.extend(self.barrier)
                w.epoch = self.epoch
                w.r = {}
                w.w = None
            d.append(w.w)
            d.extend(w.r.values())
        return d

    def mark(self, t, reads, writes):
        for r in reads:
            if r.scr and r.epoch < self.epoch:
                r.epoch = self.epoch
                r.r = {}
                r.w = None
            k = id(t.sem)
            if k not in r.r or r.r[k].val < t.val:
                r.r[k] = t
        for w in writes:
            w.w = t
            w.r = {}

    def op(self, eng, fn, reads=(), writes=()):
        eng.wait(self.deps(reads, writes))
        t = eng.emit(fn)
        self.mark(t, reads, writes)
        return t

    def dma(self, q, ds, out, in_, reads=(), writes=(), extra=(), scr=False, store=False):
        q.wait(self.deps(reads, writes) + list(extra))
        ds.cnt += 16
        q.prog.append(lambda e, o=out, i=in_, s=ds.h: e.dma_start(out=o, in_=i).then_inc(s, 16))
        t = Ticket(ds.h, ds.cnt, None)
        self.mark(t, reads, writes)
        if scr:
            self.scr_dma[id(ds.h)] = t
        if store:
            self.pending_stores[id(ds.h)] = t
        return t

    def dma_multi(self, q, ds, pairs, reads=(), writes=(), extra=(), scr=False, store=False):
        q.wait(self.deps(reads, writes) + list(extra))
        for out, in_ in pairs:
            ds.cnt += 16
            q.prog.append(lambda e, o=out, i=in_, s=ds.h: e.dma_start(out=o, in_=i).then_inc(s, 16))
        t = Ticket(ds.h, ds.cnt, None)
        self.mark(t, reads, writes)
        if scr:
            self.scr_dma[id(ds.h)] = t
        if store:
            self.pending_stores[id(ds.h)] = t
        return t

    def phase_barrier(self):
        self.epoch += 1
        self.barrier = [e.last for e in (self.pe, self.act, self.dve)] + list(self.scr_dma.values())

    def build(self):
        c, nc = self.c, self.nc
        KC, T, H, G, TPC = c.KC, c.T, c.H, c.G, c.TPC
        self.pieces, self.index = piece_plan(c)
        self.xT = nc.dram_tensor("xT", [c.D, TPC], F32, kind="ExternalInput")
        self.pT = nc.dram_tensor("pT", [c.L * c.PLE, TPC], F32, kind="ExternalInput")
        self.cs_d = nc.dram_tensor("cs", [128, c.NCS], F32, kind="ExternalInput")
        self.cc_d = nc.dram_tensor("cc", [128, c.NCC], F32, kind="ExternalInput")
        self.cp_d = nc.dram_tensor("cp", [128, c.L * c.NPL], F32, kind="ExternalInput")
        self.oT = nc.dram_tensor("oT", [c.D, TPC], F32, kind="ExternalOutput")
        self.wp, self.wb, self.wg = [], [], []
        for i, p in enumerate(self.pieces):
            rows = p["size"] // 2048
            self.wp.append(nc.dram_tensor("wp%d" % i, [rows // 8, 2048], F32, kind="ExternalInput"))
            self.wb.append(nc.dram_tensor("wb%d" % i, [rows // 8, 2048], BF16))
            self.wg.append(nc.dram_tensor("wg%d" % i, [rows, 2048], BF16))
        self.xs = nc.dram_tensor("xs", [c.D, TPC], F32)
        self.qs = nc.dram_tensor("qs", [H * 128, TPC], BF16)
        self.ys = nc.dram_tensor("ys", [G * 128, TPC], BF16)
        self.kl = [nc.dram_tensor("kl%d" % l, [H * 128, TPC], BF16) for l in range(c.L)]
        self.vl = [nc.dram_tensor("vl%d" % l, [H * 128, TPC], BF16) for l in range(c.L)]
        self.hl = [nc.dram_tensor("hl%d" % l, [128, c.NSQ * G * 8], F32) for l in range(c.L)]
        self.kg = [nc.dram_tensor("kg%d" % l, [c.RPS * H * 128, TPC], BF16) for l in range(c.L)]
        self.vg = [nc.dram_tensor("vg%d" % l, [c.RPS * H * 128, TPC], BF16) for l in range(c.L)]
        self.hg = [nc.dram_tensor("hg%d" % l, [c.RPS * 128, c.NSQ * G * 8], F32) for l in range(c.L)]

        with self.es:
            self.alloc()
            self.emit_all()
            for h in self.all_sems:
                nc.gpsimd.sem_clear(h)
            nc.all_engine_barrier()
            with nc.Block() as block:
                for name, eng in (("sync", self.sp), ("tensor", self.pe), ("scalar", self.act),
                                  ("vector", self.dve), ("gpsimd", self.pool)):
                    def run(e, eng=eng):
                        for f in eng.prog:
                            f(e)
                    getattr(block, name)(run)
        return nc

    def alloc(self):
        c = self.c
        KC, T = c.KC, c.T
        self.sp = Eng("sp", [self.sem("s_sp")])
        self.pe = Eng("pe", [self.sem("s_pe%d" % i) for i in range(8)])
        self.act = Eng("act", [self.sem("s_act%d" % i) for i in range(4)])
        self.dve = Eng("dve", [self.sem("s_dve%d" % i) for i in range(3)])
        self.pool = Eng("pool", [self.sem("s_pool")])
        self.xt = self.sb("xt", [128, KC, T], F32)
        self.ht = self.sb("ht", [128, KC, T], BF16)
        self.scr = self.sb("scr", [128, c.SCRB // 2], BF16)
        self.scr32 = self.scr.bitcast(F32)
        self.ring = self.sb("ring", [128, c.NB, c.SLABW], BF16)
        self.cs = self.sb("cs_t", [128, c.NCS], F32)
        self.cc = self.sb("cc_t", [128, c.NCC], F32)
        self.cp = self.sb("cp_t", [128, c.L * c.NPL], F32)
        self.dv = self.sb("dv_t", [128, c.L * 4], F32)
        self.onesb = self.sb("onesb", [128, 128], BF16)
        self.sg = self.sb("sg", [128, 2, T], F32)
        self.rs = self.sb("rs", [128, 2, T], F32)
        self.halo = self.sb("halo", [128, c.G, 2], F32)
        self.cvst = self.sb("cvst", [128, c.NSQ, c.G, 2], F32)
        self.bst = self.sb("bst", [128, c.NSQ, c.G, 2], F32)
        self.hgs = self.sb("hgs", [128, c.RPS, c.G, 8], F32)
        self.fx = self.sb("fx", [128, 6, c.G, 2], F32)
        self.hst = self.sb("hst", [128, c.NSQ, c.G, 8], F32)
        self.ps = self.es.enter_context(self.nc.psum_tensor("ps", [128, 8, 512], F32))
        self.bank = [Res() for _ in range(8)]
        self.xt_r = [Res() for _ in range(KC)]
        self.ht_r = [Res() for _ in range(KC)]
        self.hid_r = [Res(scr=True) for _ in range(c.FPART)]
        self.slot_r = [Res() for _ in range(c.NB)]
        self.slot_s = [self.dsem("s_slot%d" % i) for i in range(c.NB)]
        self.sg_r = [Res(), Res()]
        self.rs_r = [Res(), Res()]
        self.rs_i = 0
        self.small_r = Res()
        self.s_x = self.dsem("s_x")
        self.s_c = self.dsem("s_c")
        self.s_misc = self.dsem("s_misc")
        self.s_cast = [self.dsem("s_cast0"), self.dsem("s_cast1")]
        self.s_cc = self.sem("s_cc")
        self.cc_cnt = 0
        self.piece_t = {}
        self.n_slab = 0
        self.slab_use = {}
        self.stream = deque()
        self.scr_off = 0
        self.rot = {}

    def scr_reset(self):
        self.scr_off = 0

    def tmp32(self, n):
        o = (self.scr_off + 31) // 32 * 32
        self.scr_off = o + 4 * n
        assert self.scr_off <= self.c.SCRB, "scr overflow"
        return self.scr32[:, o // 4:o // 4 + n]

    def tmp16(self, n):
        o = (self.scr_off + 31) // 32 * 32
        self.scr_off = o + 2 * n
        assert self.scr_off <= self.c.SCRB, "scr overflow"
        return self.scr[:, o // 2:o // 2 + n]

    def rotate(self, key, n):
        i = self.rot.get(key, 0)
        self.rot[key] = (i + 1) % n
        return i

    def push_weight(self, l, key, sub=None):
        pi, off, w = self.index[(l, key)]
        if sub is None:
            self.stream.append(dict(kind="w", piece=pi, base=off, co=0, ps=w, w=w, tag=(l, key)))
        else:
            co, sw = sub
            self.stream.append(dict(kind="w", piece=pi, base=off, co=co, ps=w, w=sw, tag=(l, key, sub)))

    def push_kv(self, l, r, h, b, ncols):
        self.stream.append(dict(kind="kv", l=l, r=r, h=h, b=b, n=ncols, tag=("kv", l, r, h, b)))

    def pump(self):
        c = self.c
        while self.stream and self.n_issued - self.n_consumed < c.NB:
            d = self.stream[0]
            if d["kind"] == "w" and d["piece"] not in self.piece_t:
                return
            if d["kind"] == "kv" and d["l"] not in self.kv_t:
                return
            self.stream.popleft()
            n = self.n_issued
            self.n_issued += 1
            s = n % c.NB
            res, ds = self.slot_r[s], self.slot_s[s]
            q = self.sp
            if d["kind"] == "w":
                flat = self.wg[d["piece"]].ap().rearrange("r c -> (r c)")
                src = flat[d["base"]:d["base"] + 128 * d["ps"]].rearrange("(p w) -> p w", w=d["ps"])
                if d["w"] != d["ps"]:
                    src = src[:, d["co"]:d["co"] + d["w"]]
                self.dma(q, ds, self.ring[:, s, 0:d["w"]], src, writes=[res],
                         extra=[self.piece_t[d["piece"]]])
            else:
                l, r, h, ncol = d["l"], d["r"], d["h"], d["n"]
                c0 = d["b"] * c.TPS
                if r is None:
                    ksrc = self.kl[l].ap()[h * 128:(h + 1) * 128, c0:c0 + ncol]
                    vsrc = self.vl[l].ap()[h * 128:(h + 1) * 128, c0:c0 + ncol]
                else:
                    row = (r * c.H + h) * 128
                    ksrc = self.kg[l].ap()[row:row + 128, c0:c0 + ncol]
                    vsrc = self.vg[l].ap()[row:row + 128, c0:c0 + ncol]
                self.dma_multi(q, ds, [(self.ring[:, s, 0:ncol], ksrc), (self.ring[:, s, c.TPS:c.TPS + ncol], vsrc)],
                               writes=[res], extra=[self.kv_t[l]])
            self.slab_tags.append(d["tag"])

    def next_slab(self, tag):
        c = self.c
        n = self.n_consumed_next
        self.n_consumed_next += 1
        if self.n_issued <= n:
            self.pump()
        assert self.n_issued > n, ("slab stream stalled", tag, len(self.stream))
        assert self.slab_tags[n] == tag, (self.slab_tags[n], tag)
        return n % c.NB

    def slab_done(self):
        self.n_consumed += 1
        self.pump()

    def mm_group(self, bank_i, mms, reads, ncols, first=True, signal=True, last=True):
        bres = self.bank[bank_i]
        deps = self.deps(reads, [bres] if first else [])
        self.pe.wait(deps)
        n = len(mms)
        out = self.ps[:, bank_i, 0:ncols]
        t = None
        for i, (l, r) in enumerate(mms):
            st = first and i == 0
            fn = (lambda e, l=l, r=r, st=st, sp=(last and i == n - 1), o=out:
                  e.matmul(o, l, r, start=st, stop=sp))
            if i == n - 1 and signal:
                t = self.pe.emit(fn, True)
            else:
                self.pe.emit(fn, False)
        if t is not None:
            self.mark(t, reads, [bres])
        return t

    def emit_all(self):
        c = self.c
        L, NT = c.L, c.NT
        self.n_issued = self.n_consumed = self.n_consumed_next = 0
        self.slab_tags = []
        self.kv_t = {}
        sp, dve = self.sp, self.dve
        t0 = self.dma(sp, self.s_c, self.cs[:, :], self.cs_d.ap())
        t1 = self.dma(sp, self.s_c, self.cc[:, :], self.cc_d.ap())
        t2 = self.dma(sp, self.s_c, self.cp[:, :], self.cp_d.ap())
        for e in (self.pe, self.act, self.dve, self.pool):
            e.wait([t2])
        self.op(dve, lambda e: e.memset(self.onesb[:, :], 1.0))
        self.op(dve, lambda e: e.memset(self.halo[:, :, :], 0.0), writes=[self.small_r])
        self.op(dve, lambda e: e.memset(self.hst[:, :, :, :], 0.0), writes=[self.small_r])
        self.setup_params()
        self.pool_weights(0)
        self.plan_stream()
        for l in range(L):
            if l == 0:
                for t in range(NT):
                    self.load_x(self.xT, t)
                    self.phase_a(l, t)
            self.end_phase_a(l)
            if l + 1 < L:
                self.pool_weights(l + 1)
            for t in range(NT):
                self.load_x(self.xs, t)
                self.phase_b(l, t)
                if l + 1 < L:
                    self.phase_a(l + 1, t)
                else:
                    self.store_x(self.oT, t)
        self.sp.wait(list(self.pending_stores.values()))
        assert not self.stream and self.n_consumed == self.n_issued, (len(self.stream), self.n_consumed, self.n_issued)

    def cpc(self, l, off, n=1):
        b = l * self.c.NPL + off
        return self.cp[:, b:b + n]

    def setup_params(self):
        c = self.c
        KC = c.KC
        dve, act = self.dve, self.act
        o_gq, o_gk, o_gs, o_gc = 5 * KC, 5 * KC + 1, 5 * KC + 2, 5 * KC + 3
        self.o_gq, self.o_gk, self.o_gs, self.o_gc = o_gq, o_gk, o_gs, o_gc
        self.o_cw = 5 * KC + 4
        o_la = 5 * KC + 4 + 3 * c.G
        dvr = Res()
        for l in range(c.L):
            d = self.dv
            li = c.lam_init(l)
            self.op(dve, lambda e, l=l: e.tensor_scalar(out=d[:, 4 * l:4 * l + 1], in0=self.cpc(l, o_gq), scalar1=0.125,
                                                      scalar2=None, op0=ALU.mult), writes=[dvr])
            self.op(dve, lambda e, l=l, li=li: e.tensor_scalar(out=d[:, 4 * l + 1:4 * l + 2], in0=self.cpc(l, o_gs),
                                                             scalar1=1.0 - li, scalar2=None, op0=ALU.mult), writes=[dvr])
            self.op(dve, lambda e, l=l: e.tensor_tensor(out=d[:, 4 * l + 3:4 * l + 4], in0=self.cpc(l, o_la),
                                                      in1=self.cpc(l, o_la + 1), op=ALU.mult), writes=[dvr])
            prod = d[:, 4 * l + 3:4 * l + 4]
            self.mm_group(0, [(self.cs[:, c.cs_lo:c.cs_lo + 128], prod)], [dvr], 1)
            self.mm_group(1, [(self.cs[:, c.cs_hi:c.cs_hi + 128], prod)], [dvr], 1)
            e1 = self.rs[:, 0, 0:1]
            e2 = self.rs[:, 0, 1:2]
            self.op(act, lambda e, e1=e1: e.activation(out=e1, in_=self.ps[:, 0, 0:1], func=AF.Exp),
                    reads=[self.bank[0]], writes=[self.rs_r[0]])
            self.op(act, lambda e, e2=e2: e.activation(out=e2, in_=self.ps[:, 1, 0:1], func=AF.Exp),
                    reads=[self.bank[1]], writes=[self.rs_r[0]])
            self.op(dve, lambda e, l=l, e1=e1, e2=e2: e.tensor_tensor(out=d[:, 4 * l + 2:4 * l + 3], in0=e2, in1=e1,
                                                                    op=ALU.subtract), reads=[self.rs_r[0]], writes=[dvr])
            self.op(dve, lambda e, l=l, li=li: e.tensor_scalar(out=d[:, 4 * l + 2:4 * l + 3], in0=d[:, 4 * l + 2:4 * l + 3],
                                                             scalar1=-li, scalar2=None, op0=ALU.add), writes=[dvr])
        self.act.wait([self.dve.last])
        self.pe.wait([self.dve.last])

    def pool_weights(self, l):
        c = self.c
        pool = self.pool
        idx = [i for i, p in enumerate(self.pieces) if p["layer"] == l]
        casts = {}

        def cast(i):
            ds = self.s_cast[i % 2]
            rows = self.pieces[i]["size"] // 2048 // 8
            step = 8192
            t = None
            for r0 in range(0, rows, step):
                r1 = min(rows, r0 + step)
                t = self.dma(pool, ds, self.wb[i].ap()[r0:r1, :], self.wp[i].ap()[r0:r1, :])
            casts[i] = t

        def gather(i):
            pool.wait([casts[i]])
            self.cc_cnt += 1
            pool.prog.append(lambda e, i=i: e.collective_compute(
                "AllGather", ALU.bypass, replica_groups=[list(range(c.NC))],
                ins=[self.wb[i].ap().opt()], outs=[self.wg[i].ap().opt()]).then_inc(self.s_cc))
            self.piece_t[i] = Ticket(self.s_cc, self.cc_cnt, None)
            pool.wait([self.piece_t[i]])

        for i in idx:
            cast(i)
            gather(i)

    def end_phase_a(self, l):
        c = self.c
        hl_v = self.hl[l].ap().rearrange("p (b g c) -> p b g c", c=8, g=c.G)
        self.dma(self.sp, self.s_misc, hl_v, self.hst[:, :, :, :], reads=[self.small_r], store=True)
        st = list(self.pending_stores.values())
        self.sp.wait(st)
        self.pool.wait(st)
        groups = [list(range(c.NC))]
        for src, dst in ((self.kl[l], self.kg[l]), (self.vl[l], self.vg[l]), (self.hl[l], self.hg[l])):
            self.cc_cnt += 1
            self.pool.prog.append(lambda e, src=src, dst=dst: e.collective_compute(
                "AllGather", ALU.bypass, replica_groups=groups,
                ins=[src.ap().opt()], outs=[dst.ap().opt()]).then_inc(self.s_cc))
            self.pool.wait([Ticket(self.s_cc, self.cc_cnt, None)])
        self.kv_t[l] = Ticket(self.s_cc, self.cc_cnt, None)
        self.pump()

    def plan_stream(self):
        c = self.c
        for l in range(c.L):
            if l == 0:
                for t in range(c.NT):
                    self.plan_a(l)
            for t in range(c.NT):
                self.plan_b(l, t)
                if l + 1 < c.L:
                    self.plan_a(l + 1)

    def plan_ffn(self, l, ff):
        c = self.c
        for (f0, n) in c.parts:
            for fc in range(f0, f0 + n):
                self.push_weight(l, ("g", ff, fc))
                self.push_weight(l, ("u", ff, fc))
            for oc in range(c.KC):
                self.push_weight(l, ("d", ff, f0, n, oc))

    def plan_a(self, l):
        c = self.c
        self.plan_ffn(l, 1)
        for h in range(c.H):
            self.push_weight(l, ("q", h))
        for h in range(c.H):
            self.push_weight(l, ("k", h))
        for vg in range(c.VG):
            for kq in range(c.VKQ):
                self.push_weight(l, ("v", vg, kq))
        for g in range(c.G):
            for nm in ("cB", "cC", "cU"):
                self.push_weight(l, (nm, g))

    def plan_b(self, l, t):
        c = self.c
        b, t = divmod(t, c.NTS)
        for h in range(c.H):
            for r in range(c.RPS - 1):
                self.push_kv(l, r, h, b, c.TPS)
            self.push_kv(l, None, h, b, (t + 1) * c.T)
        for oc in range(c.KC):
            self.push_weight(l, ("o", oc))
        self.plan_ffn(l, 2)
        for s in range(c.PS):
            self.push_weight(l, ("pp", s))
        for oc in range(c.KC):
            self.push_weight(l, ("pg", oc))
            s, o = divmod(oc, c.PO)
            self.push_weight(l, ("pp", s), sub=(o * c.PKC * 128, c.PKC * 128))

    def xview(self, dram, t):
        T = self.c.T
        return dram.ap()[:, t * T:(t + 1) * T].rearrange("(kc p) t -> p kc t", p=128)

    def load_x(self, dram, t):
        v = self.xview(dram, t)
        KC = self.c.KC
        pairs = [(self.xt[:, k:min(KC, k + 8), :], v[:, k:min(KC, k + 8), :]) for k in range(0, KC, 8)]
        self.dma_multi(self.sp, self.s_x, pairs, writes=self.xt_r)

    def store_x(self, dram, t):
        v = self.xview(dram, t)
        KC = self.c.KC
        pairs = [(v[:, k:min(KC, k + 8), :], self.xt[:, k:min(KC, k + 8), :]) for k in range(0, KC, 8)]
        self.dma_multi(self.sp, self.s_x, pairs, reads=self.xt_r, store=True)

    def big_norm(self, l, goff):
        c = self.c
        KC, T = c.KC, c.T
        act, dve = self.act, self.dve
        for kc in range(KC):
            self.op(act, lambda e, kc=kc: e.activation(out=self.ht[:, kc, :], in_=self.xt[:, kc, :], func=AF.Square),
                    reads=[self.xt_r[kc]], writes=[self.ht_r[kc]])
            self.mm_group(6, [(self.onesb[:, :], self.ht[:, kc, :])], [self.ht_r[kc]], T, first=(kc == 0),
                          signal=True, last=(kc == KC - 1))
        i = self.rotate("rs", 2)
        rs = self.rs[:, i, :]
        self.op(act, lambda e: e.activation(out=rs, in_=self.ps[:, 6, 0:T], func=AF.Sqrt,
                                            bias=self.cs[:, c.cs_eps:c.cs_eps + 1], scale=1.0 / c.D),
                reads=[self.bank[6]], writes=[self.rs_r[i]])
        self.op(dve, lambda e: e.reciprocal(out=rs, in_=rs), writes=[self.rs_r[i]])
        for kc in range(KC):
            self.op(dve, lambda e, kc=kc: e.scalar_tensor_tensor(
                out=self.ht[:, kc, :], in0=self.xt[:, kc, :], scalar=self.cpc(l, goff + kc), in1=rs,
                op0=ALU.mult, op1=ALU.mult), reads=[self.xt_r[kc], self.rs_r[i]], writes=[self.ht_r[kc]])

    def small_rstd(self, sq_ap, sq_res, mat_off, n, nb_bank):
        c = self.c
        act, dve = self.act, self.dve
        self.mm_group(nb_bank, [(self.cs[:, mat_off:mat_off + 128], sq_ap)], [sq_res], n)
        i = self.rotate("rs", 2)
        rs = self.rs[:, i, 0:n]
        self.op(act, lambda e: e.activation(out=rs, in_=self.ps[:, nb_bank, 0:n], func=AF.Sqrt,
                                            bias=self.cs[:, c.cs_eps:c.cs_eps + 1], scale=1.0),
                reads=[self.bank[nb_bank]], writes=[self.rs_r[i]])
        self.op(dve, lambda e: e.reciprocal(out=rs, in_=rs), writes=[self.rs_r[i]])
        return rs, self.rs_r[i]

    def ffn(self, l, ff):
        c = self.c
        KC, T = c.KC, c.T
        act, dve = self.act, self.dve
        self.phase_barrier()
        for (f0, n) in c.parts:
            for j in range(n):
                fc = f0 + j
                pi = self.rotate("ffn_gu", 2)
                gb, ub = pi, 2 + pi
                s = self.next_slab((l, ("g", ff, fc)))
                self.mm_group(gb, [(self.ring[:, s, kc * 128:(kc + 1) * 128], self.ht[:, kc, :]) for kc in range(KC)],
                              [self.slot_r[s]] + self.ht_r, T)
                self.slab_done()
                s = self.next_slab((l, ("u", ff, fc)))
                self.mm_group(ub, [(self.ring[:, s, kc * 128:(kc + 1) * 128], self.ht[:, kc, :]) for kc in range(KC)],
                              [self.slot_r[s]] + self.ht_r, T)
                self.slab_done()
                si = self.rotate("sg", 2)
                self.op(act, lambda e, si=si, gb=gb: e.activation(out=self.sg[:, si, :], in_=self.ps[:, gb, 0:T], func=AF.Silu),
                        reads=[self.bank[gb]], writes=[self.sg_r[si]])
                self.op(dve, lambda e, si=si, ub=ub, j=j: e.tensor_tensor(
                    out=self.scr[:, j * T:(j + 1) * T], in0=self.sg[:, si, :], in1=self.ps[:, ub, 0:T], op=ALU.mult),
                    reads=[self.sg_r[si], self.bank[ub]], writes=[self.hid_r[j]])
            for oc in range(KC):
                db = 4 + self.rotate("ffn_d", 2)
                s = self.next_slab((l, ("d", ff, f0, n, oc)))
                self.mm_group(db, [(self.ring[:, s, j * 128:(j + 1) * 128], self.scr[:, j * T:(j + 1) * T]) for j in range(n)],
                              [self.slot_r[s]] + self.hid_r[:n], T)
                self.slab_done()
                self.op(dve, lambda e, db=db, oc=oc: e.scalar_tensor_tensor(
                    out=self.xt[:, oc, :], in0=self.ps[:, db, 0:T], scalar=0.5, in1=self.xt[:, oc, :],
                    op0=ALU.mult, op1=ALU.add), reads=[self.bank[db]], writes=[self.xt_r[oc]])

    def phase_a(self, l, t):
        c = self.c
        KC, T, H, G = c.KC, c.T, c.H, c.G
        act, dve, sp = self.act, self.dve, self.sp
        self.big_norm(l, 0)
        self.ffn(l, 1)
        self.big_norm(l, KC)
        self.store_x(self.xs, t)
        self.phase_barrier()
        self.scr_reset()
        sq = [self.tmp32(T) for _ in range(2)]
        sq_r = [Res(True), Res(True)]
        stg = [self.tmp16(T) for _ in range(4)]
        stg_r = [Res(True) for _ in range(4)]
        if not hasattr(self, "stg_s"):
            self.stg_s = [self.dsem("s_stg%d" % i) for i in range(4)]
            self.vst_s = [self.dsem("s_vst%d" % i) for i in range(2)]
        vst = [self.tmp16(c.VW) for _ in range(2)]
        vst_r = [Res(True), Res(True)]
        csb = self.tmp32(T)
        csb_r = Res(True)
        cu = [self.tmp32(T + 2) for _ in range(2)]
        cu_r = [Res(True), Res(True)]
        cv = self.tmp32(T)
        cv_r = Res(True)
        yun = self.tmp32(T)
        yun_r = Res(True)
        for which in ("q", "k"):
            for h in range(H):
                b = self.rotate("qk_b", 2)
                s = self.next_slab((l, (which, h)))
                self.mm_group(b, [(self.ring[:, s, kc * 128:(kc + 1) * 128], self.ht[:, kc, :]) for kc in range(KC)],
                              [self.slot_r[s]] + self.ht_r, T)
                self.slab_done()
                i = self.rotate("sq", 2)
                self.op(act, lambda e, i=i, b=b: e.activation(out=sq[i], in_=self.ps[:, b, 0:T], func=AF.Square),
                        reads=[self.bank[b]], writes=[sq_r[i]])
                nb = 2 + self.rotate("qk_nb", 2)
                rs, rs_r = self.small_rstd(sq[i], sq_r[i], c.cs_blk, T, nb)
                k = self.rotate("stg", 4)
                gcol = self.dv[:, 4 * l:4 * l + 1] if which == "q" else self.cpc(l, self.o_gk)
                self.op(dve, lambda e, k=k, b=b, gcol=gcol, rs=rs: e.scalar_tensor_tensor(
                    out=stg[k], in0=self.ps[:, b, 0:T], scalar=gcol, in1=rs, op0=ALU.mult, op1=ALU.mult),
                    reads=[self.bank[b], rs_r], writes=[stg_r[k]])
                dst = (self.qs if which == "q" else self.kl[l]).ap()[h * 128:(h + 1) * 128, t * T:(t + 1) * T]
                self.dma(sp, self.stg_s[k], dst, stg[k], reads=[stg_r[k]], scr=True, store=True)
        for vg in range(c.VG):
            for kq in range(c.VKQ):
                s = self.next_slab((l, ("v", vg, kq)))
                for ts in range(c.TS):
                    mms = [(self.ht[:, kq * c.VKB + kc, ts * 128:(ts + 1) * 128],
                            self.ring[:, s, kc * c.VW:(kc + 1) * c.VW]) for kc in range(c.VKB)]
                    bres = self.bank[4 + ts]
                    first = (kq == 0)
                    deps = self.deps([self.slot_r[s]] + self.ht_r, [bres] if first else [])
                    self.pe.wait(deps)
                    out = self.ps[:, 4 + ts, 0:c.VW]
                    tk = None
                    for i2, (lh, rh) in enumerate(mms):
                        st = first and i2 == 0
                        spf = (kq == c.VKQ - 1) and (i2 == len(mms) - 1)
                        fn = lambda e, lh=lh, rh=rh, st=st, spf=spf, out=out: e.matmul(out, lh, rh, start=st, stop=spf)
                        if i2 == len(mms) - 1:
                            tk = self.pe.emit(fn, True)
                        else:
                            self.pe.emit(fn, False)
                    self.mark(tk, [self.slot_r[s]] + self.ht_r, [bres])
                self.slab_done()
            for ts in range(c.TS):
                k = self.rotate("vst", 2)
                eng = act if ts % 2 == 0 else dve
                if eng is act:
                    self.op(act, lambda e, k=k, ts=ts: e.activation(out=vst[k], in_=self.ps[:, 4 + ts, 0:c.VW], func=AF.Copy),
                            reads=[self.bank[4 + ts]], writes=[vst_r[k]])
                else:
                    self.op(dve, lambda e, k=k, ts=ts: e.tensor_copy(out=vst[k], in_=self.ps[:, 4 + ts, 0:c.VW]),
                            reads=[self.bank[4 + ts]], writes=[vst_r[k]])
                dst = self.vl[l].ap().rearrange("(h p) (c d) -> p h c d", p=128, d=128)[
                    :, vg * c.HPG:(vg + 1) * c.HPG, t * c.TS + ts, :]
                self.dma(sp, self.vst_s[k], dst, vst[k].rearrange("p (h d) -> p h d", d=128), reads=[vst_r[k]],
                         scr=True, store=True)
        bq, tq = divmod(t, c.NTS)
        if tq == 0:
            self.op(dve, lambda e: e.memset(self.halo[:, :, :], 0.0), writes=[self.small_r])
        for g in range(G):
            b0 = 3 * self.rotate("cv_b", 2)
            bB, bC, bU = b0, b0 + 1, b0 + 2
            for nm, b in (("cB", bB), ("cC", bC), ("cU", bU)):
                s = self.next_slab((l, (nm, g)))
                self.mm_group(b, [(self.ring[:, s, kc * 128:(kc + 1) * 128], self.ht[:, kc, :]) for kc in range(KC)],
                              [self.slot_r[s]] + self.ht_r, T)
                self.slab_done()
            self.op(act, lambda e, bC=bC: e.activation(out=csb, in_=self.ps[:, bC, 0:T], func=AF.Copy),
                    reads=[self.bank[bC]], writes=[csb_r])
            ci = self.rotate("cu", 2)
            cub = cu[ci]
            self.op(dve, lambda e, cub=cub, bU=bU: e.tensor_tensor(out=cub[:, 2:T + 2], in0=csb, in1=self.ps[:, bU, 0:T], op=ALU.mult),
                    reads=[csb_r, self.bank[bU]], writes=[cu_r[ci]])
            self.op(dve, lambda e, cub=cub, g=g: e.tensor_copy(out=cub[:, 0:2], in_=self.halo[:, g, :]),
                    reads=[self.small_r], writes=[cu_r[ci]])
            w = [self.cpc(l, self.o_cw + 3 * g + j) for j in range(3)]
            self.op(dve, lambda e, cub=cub, w=w: e.tensor_scalar(out=cv, in0=cub[:, 0:T], scalar1=w[0], scalar2=None, op0=ALU.mult),
                    reads=[cu_r[ci]], writes=[cv_r])
            self.op(dve, lambda e, cub=cub, w=w: e.scalar_tensor_tensor(out=cv, in0=cub[:, 1:T + 1], scalar=w[1], in1=cv,
                                                                      op0=ALU.mult, op1=ALU.add), reads=[cu_r[ci]], writes=[cv_r])
            self.op(dve, lambda e, cub=cub, w=w: e.scalar_tensor_tensor(out=cv, in0=cub[:, 2:T + 2], scalar=w[2], in1=cv,
                                                                      op0=ALU.mult, op1=ALU.add), reads=[cu_r[ci]], writes=[cv_r])
            self.op(dve, lambda e, cub=cub, g=g: e.tensor_copy(out=self.halo[:, g, :], in_=cub[:, T:T + 2]),
                    reads=[cu_r[ci]], writes=[self.small_r])
            if tq == 0:
                self.op(dve, lambda e, g=g: e.tensor_copy(out=self.cvst[:, bq, g, :], in_=cv[:, 0:2]), reads=[cv_r], writes=[self.small_r])
                self.op(dve, lambda e, g=g, bB=bB: e.tensor_copy(out=self.bst[:, bq, g, :], in_=self.ps[:, bB, 0:2]),
                        reads=[self.bank[bB]], writes=[self.small_r])
            if tq == c.NTS - 1:
                self.op(dve, lambda e, g=g: e.tensor_copy(out=self.hst[:, bq, g, 0:2], in_=self.halo[:, g, :]),
                        reads=[self.small_r], writes=[self.small_r])
            self.op(dve, lambda e, bB=bB: e.tensor_tensor(out=yun, in0=cv, in1=self.ps[:, bB, 0:T], op=ALU.mult),
                    reads=[cv_r, self.bank[bB]], writes=[yun_r])
            i = self.rotate("sq", 2)
            self.op(act, lambda e, i=i: e.activation(out=sq[i], in_=yun, func=AF.Square), reads=[yun_r], writes=[sq_r[i]])
            nb = 6 + self.rotate("cv_nb", 2)
            rs, rs_r = self.small_rstd(sq[i], sq_r[i], c.cs_avg, T, nb)
            k = self.rotate("stg", 4)
            self.op(dve, lambda e, k=k, rs=rs: e.scalar_tensor_tensor(
                out=stg[k], in0=yun, scalar=self.cpc(l, self.o_gc), in1=rs, op0=ALU.mult, op1=ALU.mult),
                reads=[yun_r, rs_r], writes=[stg_r[k]])
            dst = self.ys.ap()[g * 128:(g + 1) * 128, t * T:(t + 1) * T]
            self.dma(sp, self.stg_s[k], dst, stg[k], reads=[stg_r[k]], scr=True, store=True)

    def phase_b(self, l, t):
        c = self.c
        KC, T, H, G, TS, CPR, TPC = c.KC, c.T, c.H, c.G, c.TS, c.CPR, c.TPC
        act, dve, sp, pe = self.act, self.dve, self.sp, self.pe
        self.phase_barrier()
        self.scr_reset()
        Ooff = [self.tmp32(T), self.tmp32(T)]
        doff = [self.tmp32(T), self.tmp32(T)]
        Ooff_r = [Res(True), Res(True)]
        doff_r = [Res(True), Res(True)]
        sm = [self.tmp32(T), self.tmp32(T)]
        sm_r = [Res(True), Res(True)]
        fh = self.tmp32(T)
        fh_r = Res(True)
        P = [self.tmp16(T) for _ in range(4)]
        P_r = [Res(True) for _ in range(4)]
        qb = [self.tmp16(T) for _ in range(2)]
        qb_r = [Res(True), Res(True)]
        if not hasattr(self, "qb_s"):
            self.qb_s = [self.dsem("s_qb0"), self.dsem("s_qb1")]
            self.s_y = self.dsem("s_y")
            self.s_p = self.dsem("s_p")
        ysrc = self.ys.ap()[:, t * T:(t + 1) * T].rearrange("(g p) t -> p g t", p=128)
        pairs = [(self.ht[:, H + g0:H + min(G, g0 + 8), :], ysrc[:, g0:min(G, g0 + 8), :]) for g0 in range(0, G, 8)]
        self.dma_multi(sp, self.s_y, pairs, writes=self.ht_r[H:H + G])
        bq, tq = divmod(t, c.NTS)
        if tq == 0:
            self.halo_fix(l, bq)

        def load_q(h):
            i = h % 2
            src = self.qs.ap()[h * 128:(h + 1) * 128, t * T:(t + 1) * T]
            self.dma(sp, self.qb_s[i], qb[i], src, writes=[qb_r[i]], scr=True)

        load_q(0)
        eps = self.cs[:, c.cs_eps:c.cs_eps + 1]
        for h in range(H):
            if h + 1 < H:
                load_q(h + 1)
            m = c.slopes[h]
            q, q_r = qb[h % 2], qb_r[h % 2]
            self.op(act, lambda e, m=m: e.activation(out=fh, in_=self.cs[:, c.cs_iota:c.cs_iota + T], func=AF.Exp, scale=-m),
                    writes=[fh_r])

            def scores(Kap, s):
                pi = self.rotate("att_s", 2)
                b0, b1 = 2 * pi, 2 * pi + 1
                self.mm_group(b0, [(Kap[0:64, :], q[0:64, :])], [self.slot_r[s], q_r], T)
                self.mm_group(b1, [(Kap[64:128, :], q[64:128, :])], [self.slot_r[s], q_r], T)
                return b0, b1

            def pv(Vap, s, pis, first, last):
                for mi in range(2):
                    p, p_r = P[pis[mi]], P_r[pis[mi]]
                    self.mm_group(4 + mi, [(Vap, p)], [self.slot_r[s], p_r], T, first=first, signal=False, last=last)
                    bres = self.bank[6 + mi]
                    deps = self.deps([p_r], [bres] if first else [])
                    pe.wait(deps)
                    tk = pe.emit(lambda e, p=p, mi=mi, first=first, last=last: e.matmul(
                        self.ps[:, 6 + mi, 0:T], self.onesb[:, :], p, start=first, stop=last), True)
                    self.mark(tk, [self.slot_r[s], p_r], [bres, self.bank[4 + mi]])

            first = True
            n_off = (c.RPS - 1) * CPR + tq * TS
            i_off = 0
            for r in range(c.RPS - 1):
                s = self.next_slab(("kv", l, r, h, bq))
                for cch in range(CPR):
                    Kap = self.ring[:, s, cch * 128:(cch + 1) * 128]
                    Vap = self.ring[:, s, c.TPS + cch * 128:c.TPS + (cch + 1) * 128]
                    e_i = TS * tq - cch + CPR - 1
                    col = c.cc_al + (h * (c.RPS - 1) + r) * c.NE + e_i
                    bias = self.cc[:, col:col + 1]
                    b0, b1 = scores(Kap, s)
                    pis = []
                    for mi, b in enumerate((b0, b1)):
                        k = self.rotate("P", 4)
                        pis.append(k)
                        self.op(act, lambda e, k=k, b=b, bias=bias: e.activation(out=P[k], in_=self.ps[:, b, 0:T], func=AF.Exp,
                                                                                bias=bias, scale=1.0),
                                reads=[self.bank[b]], writes=[P_r[k]])
                    i_off += 1
                    pv(Vap, s, pis, first, i_off == n_off)
                    first = False
                self.slab_done()
            s = self.next_slab(("kv", l, None, h, bq))
            for cch in range(tq * TS):
                Kap = self.ring[:, s, cch * 128:(cch + 1) * 128]
                Vap = self.ring[:, s, c.TPS + cch * 128:c.TPS + (cch + 1) * 128]
                dd = tq * TS - cch
                col = c.cs_own + h * c.NDD + dd - 1
                bias = self.cs[:, col:col + 1]
                b0, b1 = scores(Kap, s)
                pis = []
                for mi, b in enumerate((b0, b1)):
                    k = self.rotate("P", 4)
                    pis.append(k)
                    self.op(act, lambda e, k=k, b=b, bias=bias: e.activation(out=P[k], in_=self.ps[:, b, 0:T], func=AF.Exp,
                                                                            bias=bias, scale=1.0),
                            reads=[self.bank[b]], writes=[P_r[k]])
                i_off += 1
                pv(Vap, s, pis, first, i_off == n_off)
                first = False
            for mi in range(2):
                self.op(dve, lambda e, mi=mi: e.tensor_tensor(out=Ooff[mi], in0=self.ps[:, 4 + mi, 0:T], in1=fh, op=ALU.mult),
                        reads=[self.bank[4 + mi], fh_r], writes=[Ooff_r[mi]])
                self.op(dve, lambda e, mi=mi: e.tensor_tensor(out=doff[mi], in0=self.ps[:, 6 + mi, 0:T], in1=fh, op=ALU.mult),
                        reads=[self.bank[6 + mi], fh_r], writes=[doff_r[mi]])
            for i in range(TS):
                cch = tq * TS + i
                Kap = self.ring[:, s, cch * 128:(cch + 1) * 128]
                Vap = self.ring[:, s, c.TPS + cch * 128:c.TPS + (cch + 1) * 128]
                b0, b1 = scores(Kap, s)
                dbo = c.cs_db + (TS - 1 - i) * 128
                dbv = self.cs[:, dbo:dbo + T]
                pis = []
                for mi, b in enumerate((b0, b1)):
                    self.op(dve, lambda e, mi=mi, b=b, dbv=dbv, m=m: e.scalar_tensor_tensor(
                        out=sm[mi], in0=dbv, scalar=-m, in1=self.ps[:, b, 0:T], op0=ALU.mult, op1=ALU.add),
                        reads=[self.bank[b]], writes=[sm_r[mi]])
                    k = self.rotate("P", 4)
                    pis.append(k)
                    self.op(act, lambda e, k=k, mi=mi: e.activation(out=P[k], in_=sm[mi], func=AF.Exp),
                            reads=[sm_r[mi]], writes=[P_r[k]])
                pv(Vap, s, pis, i == 0, i == TS - 1)
            self.slab_done()
            for mi in range(2):
                self.op(dve, lambda e, mi=mi: e.tensor_tensor(out=Ooff[mi], in0=self.ps[:, 4 + mi, 0:T], in1=Ooff[mi], op=ALU.add),
                        reads=[self.bank[4 + mi]], writes=[Ooff_r[mi]])
                self.op(dve, lambda e, mi=mi: e.tensor_tensor(out=doff[mi], in0=self.ps[:, 6 + mi, 0:T], in1=doff[mi], op=ALU.add),
                        reads=[self.bank[6 + mi]], writes=[doff_r[mi]])
            for mi in range(2):
                self.op(dve, lambda e, mi=mi: e.reciprocal(out=doff[mi], in_=doff[mi]), writes=[doff_r[mi]])
                self.op(dve, lambda e, mi=mi: e.tensor_tensor(out=Ooff[mi], in0=Ooff[mi], in1=doff[mi], op=ALU.mult),
                        reads=[doff_r[mi]], writes=[Ooff_r[mi]])
            self.op(dve, lambda e: e.scalar_tensor_tensor(out=Ooff[0], in0=Ooff[1], scalar=self.dv[:, 4 * l + 2:4 * l + 3],
                                                          in1=Ooff[0], op0=ALU.mult, op1=ALU.add),
                    reads=[Ooff_r[1]], writes=[Ooff_r[0]])
            self.op(act, lambda e: e.activation(out=doff[0], in_=Ooff[0], func=AF.Square), reads=[Ooff_r[0]], writes=[doff_r[0]])
            nb = 2 * self.rotate("att_s", 2)
            rs, rs_r = self.small_rstd(doff[0], doff_r[0], c.cs_avg, T, nb)
            self.op(dve, lambda e, h=h, rs=rs: e.scalar_tensor_tensor(
                out=self.ht[:, h, :], in0=Ooff[0], scalar=self.dv[:, 4 * l + 1:4 * l + 2], in1=rs, op0=ALU.mult, op1=ALU.mult),
                reads=[Ooff_r[0], rs_r], writes=[self.ht_r[h]])
        for oc in range(KC):
            b = self.rotate("op_b", 2)
            s = self.next_slab((l, ("o", oc)))
            self.mm_group(b, [(self.ring[:, s, kc * 128:(kc + 1) * 128], self.ht[:, kc, :]) for kc in range(KC)],
                          [self.slot_r[s]] + self.ht_r, T)
            self.slab_done()
            self.op(dve, lambda e, b=b, oc=oc: e.tensor_tensor(out=self.xt[:, oc, :], in0=self.ps[:, b, 0:T], in1=self.xt[:, oc, :],
                                                             op=ALU.add), reads=[self.bank[b]], writes=[self.xt_r[oc]])
        self.big_norm(l, 2 * KC)
        self.ffn(l, 2)
        self.big_norm(l, 3 * KC)
        self.phase_barrier()
        self.scr_reset()
        PKC = c.PKC
        p32 = self.tmp32(PKC * T)
        p16 = self.tmp16(PKC * T)
        p_r = Res(True)
        sqb = [self.tmp16(T), self.tmp16(T)]
        sqb_r = [Res(True), Res(True)]
        sig = [self.tmp32(T), self.tmp32(T)]
        sig_r = [Res(True), Res(True)]
        en = [self.tmp32(T), self.tmp32(T)]
        en_r = [Res(True), Res(True)]
        rse = self.tmp32(T)
        rse_r = Res(True)
        psrc = self.pT.ap()[l * c.PLE:(l + 1) * c.PLE, t * T:(t + 1) * T].rearrange("(kc p) t -> p kc t", p=128)
        self.dma(sp, self.s_p, p32.rearrange("p (kc t) -> p kc t", t=T), psrc, writes=[p_r], scr=True)
        self.op(dve, lambda e: e.tensor_copy(out=p16, in_=p32), writes=[p_r])
        for s_i in range(c.PS):
            s = self.next_slab((l, ("pp", s_i)))
            for o in range(c.PO):
                oc = s_i * c.PO + o
                b = self.rotate("ple_e", 2)
                self.mm_group(b, [(self.ring[:, s, (o * PKC + kc) * 128:(o * PKC + kc + 1) * 128], p16[:, kc * T:(kc + 1) * T])
                                  for kc in range(PKC)], [self.slot_r[s], p_r], T)
                i = self.rotate("sqb", 2)
                self.op(act, lambda e, i=i, b=b: e.activation(out=sqb[i], in_=self.ps[:, b, 0:T], func=AF.Square),
                        reads=[self.bank[b]], writes=[sqb_r[i]])
                self.mm_group(6, [(self.onesb[:, :], sqb[i])], [sqb_r[i]], T, first=(oc == 0), last=(oc == KC - 1))
            self.slab_done()
        self.op(act, lambda e: e.activation(out=rse, in_=self.ps[:, 6, 0:T], func=AF.Sqrt, bias=eps, scale=1.0 / c.D),
                reads=[self.bank[6]], writes=[rse_r])
        self.op(dve, lambda e: e.reciprocal(out=rse, in_=rse), writes=[rse_r])
        for oc in range(KC):
            gb = 2 + self.rotate("ple_g", 2)
            s = self.next_slab((l, ("pg", oc)))
            self.mm_group(gb, [(self.ring[:, s, kc * 128:(kc + 1) * 128], self.ht[:, kc, :]) for kc in range(KC)],
                          [self.slot_r[s]] + self.ht_r, T)
            self.slab_done()
            s_i, o = divmod(oc, c.PO)
            s = self.next_slab((l, ("pp", s_i), (o * PKC * 128, PKC * 128)))
            b = self.rotate("ple_e", 2)
            self.mm_group(b, [(self.ring[:, s, kc * 128:(kc + 1) * 128], p16[:, kc * T:(kc + 1) * T]) for kc in range(PKC)],
                          [self.slot_r[s], p_r], T)
            self.slab_done()
            i = self.rotate("sig", 2)
            self.op(act, lambda e, i=i, gb=gb: e.activation(out=sig[i], in_=self.ps[:, gb, 0:T], func=AF.Sigmoid),
                    reads=[self.bank[gb]], writes=[sig_r[i]])
            self.op(dve, lambda e, i=i, b=b, oc=oc: e.scalar_tensor_tensor(
                out=en[i], in0=self.ps[:, b, 0:T], scalar=self.cpc(l, 4 * KC + oc), in1=rse, op0=ALU.mult, op1=ALU.mult),
                reads=[self.bank[b], rse_r], writes=[en_r[i]])
            self.op(dve, lambda e, i=i: e.tensor_tensor(out=en[i], in0=en[i], in1=sig[i], op=ALU.mult),
                    reads=[sig_r[i]], writes=[en_r[i]])
            self.op(dve, lambda e, i=i, oc=oc: e.tensor_tensor(out=self.xt[:, oc, :], in0=en[i], in1=self.xt[:, oc, :], op=ALU.add),
                    reads=[en_r[i]], writes=[self.xt_r[oc]])

    def halo_fix(self, l, bq):
        c = self.c
        G, H, RPS = c.G, c.H, c.RPS
        dve, act, sp = self.dve, self.act, self.sp
        src = self.hg[l].ap().rearrange("(r p) (b g c) -> p r b g c", p=128, c=8, g=G)[:, :, bq, :, :]
        self.dma(sp, self.s_misc, self.hgs[:, :, :, :], src, writes=[self.small_r], extra=[self.kv_t[l]])
        hp = self.fx[:, 0, :, :]
        t0 = self.fx[:, 1, :, :]
        cf = self.fx[:, 2, :, :]
        yu = self.fx[:, 3, :, :]
        sqv = self.fx[:, 4, :, :]
        sr = self.small_r
        for r in range(RPS):
            sel = self.cc[:, c.cc_sel + r:c.cc_sel + r + 1]
            if r == 0:
                self.op(dve, lambda e, sel=sel: e.tensor_scalar(out=hp, in0=self.hgs[:, 0, :, 0:2], scalar1=sel, scalar2=None,
                                                                op0=ALU.mult), reads=[sr], writes=[sr])
            else:
                self.op(dve, lambda e, sel=sel, r=r: e.scalar_tensor_tensor(out=hp, in0=self.hgs[:, r, :, 0:2], scalar=sel, in1=hp,
                                                                          op0=ALU.mult, op1=ALU.add), reads=[sr], writes=[sr])
        for g in range(G):
            w0 = self.cpc(l, self.o_cw + 3 * g + 0)
            w1 = self.cpc(l, self.o_cw + 3 * g + 1)
            self.op(dve, lambda e, g=g, w0=w0: e.scalar_tensor_tensor(out=cf[:, g, 0:1], in0=hp[:, g, 0:1], scalar=w0,
                                                                    in1=self.cvst[:, bq, g, 0:1], op0=ALU.mult, op1=ALU.add),
                    reads=[sr], writes=[sr])
            self.op(dve, lambda e, g=g, w1=w1: e.scalar_tensor_tensor(out=cf[:, g, 0:1], in0=hp[:, g, 1:2], scalar=w1,
                                                                    in1=cf[:, g, 0:1], op0=ALU.mult, op1=ALU.add),
                    reads=[sr], writes=[sr])
            self.op(dve, lambda e, g=g, w0=w0: e.scalar_tensor_tensor(out=cf[:, g, 1:2], in0=hp[:, g, 1:2], scalar=w0,
                                                                    in1=self.cvst[:, bq, g, 1:2], op0=ALU.mult, op1=ALU.add),
                    reads=[sr], writes=[sr])
        self.op(dve, lambda e: e.tensor_tensor(out=yu, in0=cf, in1=self.bst[:, bq, :, :], op=ALU.mult), reads=[sr], writes=[sr])
        self.op(dve, lambda e: e.tensor_tensor(out=sqv, in0=yu, in1=yu, op=ALU.mult), reads=[sr], writes=[sr])
        nb = 0
        rs, rs_r = self.small_rstd(sqv.rearrange("p g c -> p (g c)"), sr, c.cs_avg, 2 * G, nb)
        self.op(dve, lambda e, rs=rs: e.scalar_tensor_tensor(
            out=self.ht[:, H:H + G, 0:2], in0=yu, scalar=self.cpc(l, self.o_gc), in1=rs.rearrange("p (g c) -> p g c", c=2),
            op0=ALU.mult, op1=ALU.mult), reads=[sr, rs_r], writes=self.ht_r[H:H + G])


def host_consts(cfg):
    c = cfg
    cs = np.zeros((128, c.NCS), np.float32)
    cs[:, c.cs_avg:c.cs_avg + 128] = 1.0 / 128
    cs[0:64, c.cs_blk:c.cs_blk + 64] = 1.0 / 64
    cs[64:128, c.cs_blk + 64:c.cs_blk + 128] = 1.0 / 64
    cs[0:64, c.cs_lo:c.cs_lo + 128] = 1.0
    cs[64:128, c.cs_hi:c.cs_hi + 128] = 1.0
    cs[:, c.cs_iota:c.cs_iota + c.T] = np.arange(c.T, dtype=np.float32)[None, :]
    off = (c.TS - 1) * 128
    j = np.arange(c.T + off)[None, :] - off - np.arange(128)[:, None]
    cs[:, c.cs_db:c.cs_db + c.T + off] = np.where(j >= 0, j, 1e7).astype(np.float32)
    p = np.arange(128, dtype=np.float64)
    for h in range(c.H):
        for dd in range(1, c.NDD + 1):
            cs[:, c.cs_own + h * c.NDD + dd - 1] = (-c.slopes[h] * (128.0 * dd - p)).astype(np.float32)
    cs[:, c.cs_eps] = EPS
    ccs = []
    for core in range(c.NC):
        jr = core
        cc = np.zeros((128, c.NCC), np.float32)
        for h in range(c.H):
            for r in range(c.RPS - 1):
                for e in range(c.NE):
                    col = c.cc_al + (h * (c.RPS - 1) + r) * c.NE + e
                    if r < jr:
                        dch = c.CPR * (jr - r) + e - c.CPR + 1
                        cc[:, col] = (-c.slopes[h] * (128.0 * dch - p)).astype(np.float32)
                    else:
                        cc[:, col] = NEG
        for r in range(c.RPS):
            cc[:, c.cc_sel + r] = 1.0 if r == jr - 1 else 0.0
        ccs.append(cc)
    return cs, ccs


def host_params(cfg, W):
    c = cfg
    KC = c.KC
    cp = np.zeros((128, c.L * c.NPL), np.float32)
    for l in range(c.L):
        b = l * c.NPL
        for i, nm in enumerate(("ffn1_norm", "mix_norm", "ffn2_norm", "ple_gate_norm", "ple_post_norm")):
            cp[:, b + i * KC:b + (i + 1) * KC] = W[nm][l].reshape(KC, 128).T
        o = b + 5 * KC
        cp[:, o] = np.tile(W["q_norm"][l], 2)
        cp[:, o + 1] = np.tile(W["k_norm"][l], 2)
        cp[:, o + 2] = W["attn_subln"][l]
        cp[:, o + 3] = W["conv_norm"][l]
        cw = W["conv_w"][l]
        for g in range(c.G):
            for j in range(3):
                cp[:, o + 4 + 3 * g + j] = cw[j, g * 128:(g + 1) * 128]
        o2 = o + 4 + 3 * c.G
        cp[:, o2] = np.concatenate([W["lambda_q1"][l], W["lambda_q2"][l]])
        cp[:, o2 + 1] = np.concatenate([W["lambda_k1"][l], W["lambda_k2"][l]])
    return cp


def prep_inputs(cfg, W):
    c = cfg
    W = {k: np.asarray(v) for k, v in W.items()}
    pieces, _ = piece_plan(c)
    cs, ccs = host_consts(c)
    cp = host_params(c, W)
    in_maps = [dict(cs=cs, cc=ccs[i], cp=cp) for i in range(c.NC)]
    x, p = W["x"], W["p"]
    for core in range(c.NC):
        sl = slice(core * c.TPS, (core + 1) * c.TPS)
        in_maps[core]["xT"] = np.ascontiguousarray(np.concatenate([x[b, sl, :].T for b in range(c.B)], axis=1))
        in_maps[core]["pT"] = np.ascontiguousarray(np.concatenate(
            [np.concatenate([p[l, b, sl, :].T for b in range(c.B)], axis=1) for l in range(c.L)], axis=0))
    for i, pc in enumerate(pieces):
        flat = np.empty(pc["size"], np.float32)
        for key, w, off in pc["slabs"]:
            flat[off:off + 128 * w] = host_slab(c, W, pc["layer"], key).reshape(-1)
        rows = pc["size"] // 2048
        sh = flat.reshape(c.NC, rows // c.NC, 2048)
        for core in range(c.NC):
            in_maps[core]["wp%d" % i] = sh[core]
    return in_maps


def assemble(cfg, results):
    c = cfg
    out = np.empty((c.B, c.S, c.D), np.float32)
    for core in range(c.NC):
        for b in range(c.B):
            out[b, core * c.TPS:(core + 1) * c.TPS, :] = results[core]["oT"][:, b * c.TPS:(b + 1) * c.TPS].T
    return out


def run(cfg, inputs, trace=False):
    g = Gen(cfg)
    nc = g.build()
    in_maps = prep_inputs(cfg, inputs)
    res = run_bass_kernel_spmd(nc, in_maps, core_ids=list(range(cfg.NC)), trace=trace)
    return assemble(cfg, res.results), res


def kernel(**inputs):
    out, _ = run(Cfg(), inputs)
    return out
```
